# Optimizing a Trainium2 kernel written in Bass

```python
import jax, jax.numpy as jnp
from jax import lax
import numpy as np

D_MODEL = 2048
BATCH = 4
SEQ = 4096
DEPTH = 4

GRID_W = 64
CTX_LEN = 256
N_MIXERS = 3
N_MOD = 9
D_FF = 5632
RMS_EPS = 1e-6
ROPE_BASE = 10000.0
NEG_INF = -1e30

NA_HEADS = 16
NA_HEAD_DIM = 128
NA_KH = 8
NA_KW = 16

MLA_HEADS = 16
MLA_Q_LORA = 512
MLA_KV_LORA = 512
MLA_NOPE = 128
MLA_ROPE = 64
MLA_V = 128
MLA_BLOCK = 128

SWA_HEADS = 16
SWA_KV_HEADS = 4
SWA_HEAD_DIM = 128
SWA_WINDOW = 128
SWA_BLOCK = 128

kernel_name = 'hybrid_na_mla_swa_macaron_dit'


def rmsnorm(x, g):
    xf = x.astype(jnp.float32)
    y = xf * lax.rsqrt(jnp.mean(xf * xf, axis=-1, keepdims=True) + RMS_EPS)
    return (y * g.astype(jnp.float32)).astype(x.dtype)


def modulate(x, g, shift, scale):
    return rmsnorm(x, g) * (1 + scale) + shift


def adaln_params(cond, w, b):
    m = jax.nn.silu(cond) @ w + b
    return jnp.split(m, N_MOD, axis=-1)


def swiglu(h, w_in, w_out):
    gate, up = jnp.split(h @ w_in, 2, axis=-1)
    return (jax.nn.silu(gate) * up) @ w_out


def axial_rope_tables(n_tokens, rot_dim):
    t = jnp.arange(n_tokens)
    row = (t // GRID_W).astype(jnp.float32)
    col = (t % GRID_W).astype(jnp.float32)
    n_freq = rot_dim // 4
    inv_freq = ROPE_BASE ** (-jnp.arange(n_freq, dtype=jnp.float32) / n_freq)
    ang = jnp.concatenate([row[:, None] * inv_freq, col[:, None] * inv_freq], axis=-1)
    return jnp.cos(ang), jnp.sin(ang)


def apply_rope(x, cos, sin):
    half = x.shape[-1] // 2
    xf = x.astype(jnp.float32)
    x1, x2 = xf[..., :half], xf[..., half:]
    return jnp.concatenate([x1 * cos - x2 * sin, x1 * sin + x2 * cos], axis=-1).astype(x.dtype)


def softmax_with_sink(s, sink):
    m = jnp.maximum(jnp.max(s, axis=-1, keepdims=True), sink)
    e = jnp.exp(s - m)
    return e / (jnp.sum(e, axis=-1, keepdims=True) + jnp.exp(sink - m))


def context_attention(q, k, v, scale):
    s = jnp.einsum('bqhd,bkhd->bhqk', q, k).astype(jnp.float32) * scale
    p = jax.nn.softmax(s, axis=-1).astype(v.dtype)
    return jnp.einsum('bhqk,bkhd->bqhd', p, v)


def neighbourhood_attention(hx, hz, w_qkv, rpb, w_o, need_ctx):
    B, S, _ = hx.shape
    T = hz.shape[1]
    rows = S // GRID_W
    kh = min(NA_KH, rows)
    H, dh = NA_HEADS, NA_HEAD_DIM
    scale = dh ** -0.5
    qkv = (hx @ w_qkv).reshape(B, S, 3, H, dh)
    q, k, v = qkv[:, :, 0], qkv[:, :, 1], qkv[:, :, 2]
    qkvz = (hz @ w_qkv).reshape(B, T, 3, H, dh)
    qz, kz, vz = qkvz[:, :, 0], qkvz[:, :, 1], qkvz[:, :, 2]

    col = jnp.arange(GRID_W)
    col_start = jnp.clip(col - NA_KW // 2, 0, GRID_W - NA_KW)
    in_win = (col[None, :] >= col_start[:, None]) & (col[None, :] < col_start[:, None] + NA_KW)
    mask = jnp.broadcast_to(in_win[:, None, :], (GRID_W, kh, GRID_W)).reshape(GRID_W, kh * GRID_W)
    col_idx = jnp.clip(col[None, :] - col[:, None] + NA_KW - 1, 0, 2 * NA_KW - 2)
    q_rows = jnp.moveaxis(q.reshape(B, rows, GRID_W, H, dh), 1, 0)
    n_lat = kh * GRID_W

    def row_block(args):
        q_r, r = args
        rs = jnp.clip(r - kh // 2, 0, rows - kh)
        k_band = lax.dynamic_slice_in_dim(k, rs * GRID_W, n_lat, axis=1)
        v_band = lax.dynamic_slice_in_dim(v, rs * GRID_W, n_lat, axis=1)
        row_idx = rs + jnp.arange(kh) - r + NA_KH - 1
        bias = rpb[:, row_idx[:, None, None], col_idx[None, :, :]]
        bias = jnp.transpose(bias, (0, 2, 1, 3)).reshape(H, GRID_W, n_lat)
        s_lat = jnp.einsum('bqhd,bkhd->bhqk', q_r, k_band).astype(jnp.float32) * scale + bias.astype(jnp.float32)
        s_lat = jnp.where(mask, s_lat, NEG_INF)
        s_ctx = jnp.einsum('bqhd,bkhd->bhqk', q_r, kz).astype(jnp.float32) * scale
        p = jax.nn.softmax(jnp.concatenate([s_lat, s_ctx], axis=-1), axis=-1).astype(v.dtype)
        return (jnp.einsum('bhqk,bkhd->bqhd', p[..., :n_lat], v_band)
                + jnp.einsum('bhqk,bkhd->bqhd', p[..., n_lat:], vz))

    o = lax.map(row_block, (q_rows, jnp.arange(rows)))
    ox = jnp.moveaxis(o, 0, 1).reshape(B, S, H * dh) @ w_o
    oz = None
    if need_ctx:
        oz = context_attention(qz, kz, vz, scale).reshape(B, T, H * dh) @ w_o
    return ox, oz


def mla_attention(hx, hz, w_down, q_norm_g, w_q_up, kv_norm_g, w_kv_up, w_o, need_ctx):
    B, S, _ = hx.shape
    T = hz.shape[1]
    H = MLA_HEADS
    scale = (MLA_NOPE + MLA_ROPE) ** -0.5
    cos, sin = axial_rope_tables(S, MLA_ROPE)

    def project(h):
        n = h.shape[1]
        cq, ckv, k_pe = jnp.split(h @ w_down, [MLA_Q_LORA, MLA_Q_LORA + MLA_KV_LORA], axis=-1)
        q = (rmsnorm(cq, q_norm_g) @ w_q_up).reshape(B, n, H, MLA_NOPE + MLA_ROPE)
        kv = (rmsnorm(ckv, kv_norm_g) @ w_kv_up).reshape(B, n, H, MLA_NOPE + MLA_V)
        return q[..., :MLA_NOPE], q[..., MLA_NOPE:], kv[..., :MLA_NOPE], k_pe, kv[..., MLA_NOPE:]

    qn, qp, kn, kp, v = project(hx)
    qp = apply_rope(qp, cos[:, None, :], sin[:, None, :])
    kp = apply_rope(kp, cos, sin)
    qnz, qpz, knz, kpz, vz = project(hz)
    kn_all = jnp.concatenate([kn, knz], axis=1)
    kp_all = jnp.concatenate([kp, kpz], axis=1)
    v_all = jnp.concatenate([v, vz], axis=1)

    nb = S // MLA_BLOCK

    def to_blocks(t):
        return jnp.moveaxis(t.reshape(B, nb, MLA_BLOCK, *t.shape[2:]), 1, 0)

    def query_block(args):
        qn_b, qp_b = args
        s = (jnp.einsum('bqhd,bkhd->bhqk', qn_b, kn_all).astype(jnp.float32)
             + jnp.einsum('bqhd,bkd->bhqk', qp_b, kp_all).astype(jnp.float32)) * scale
        p = jax.nn.softmax(s, axis=-1).astype(v_all.dtype)
        return jnp.einsum('bhqk,bkhd->bqhd', p, v_all)

    o = lax.map(query_block, (to_blocks(qn), to_blocks(qp)))
    ox = jnp.moveaxis(o, 0, 1).reshape(B, S, H * MLA_V) @ w_o
    oz = None
    if need_ctx:
        s = (jnp.einsum('bqhd,bkhd->bhqk', qnz, knz).astype(jnp.float32)
             + jnp.einsum('bqhd,bkd->bhqk', qpz, kpz).astype(jnp.float32)) * scale
        p = jax.nn.softmax(s, axis=-1).astype(vz.dtype)
        oz = jnp.einsum('bhqk,bkhd->bqhd', p, vz).reshape(B, T, H * MLA_V) @ w_o
    return ox, oz


def window_gqa_sink(hx, hz, w_qkv, sink, w_o, need_ctx):
    B, S, _ = hx.shape
    T = hz.shape[1]
    H, KVH, dh = SWA_HEADS, SWA_KV_HEADS, SWA_HEAD_DIM
    G = H // KVH
    scale = dh ** -0.5

    def project(h):
        n = h.shape[1]
        q, k, v = jnp.split(h @ w_qkv, [H * dh, (H + KVH) * dh], axis=-1)
        return q.reshape(B, n, KVH, G, dh), k.reshape(B, n, KVH, dh), v.reshape(B, n, KVH, dh)

    q, k, v = project(hx)
    cos, sin = axial_rope_tables(S, dh)
    q = apply_rope(q, cos[:, None, None, :], sin[:, None, None, :])
    k = apply_rope(k, cos[:, None, :], sin[:, None, :])
    qz, kz, vz = project(hz)
    sink_f = sink.astype(jnp.float32).reshape(KVH, G)

    nb = S // SWA_BLOCK
    pad = ((0, 0), (SWA_BLOCK, SWA_BLOCK), (0, 0), (0, 0))

    def band(t):
        tp = jnp.pad(t, pad).reshape(B, nb + 2, SWA_BLOCK, KVH, dh)
        return jnp.concatenate([tp[:, :-2], tp[:, 1:-1], tp[:, 2:]], axis=2)

    k_band, v_band = band(k), band(v)
    q_blk = q.reshape(B, nb, SWA_BLOCK, KVH, G, dh)
    qi = jnp.arange(SWA_BLOCK)
    kj = jnp.arange(3 * SWA_BLOCK)
    rel = kj[None, :] - SWA_BLOCK - qi[:, None]
    kpos = jnp.arange(nb)[:, None] * SWA_BLOCK - SWA_BLOCK + kj[None, :]
    mask = (jnp.abs(rel) <= SWA_WINDOW)[None] & ((kpos >= 0) & (kpos < S))[:, None, :]
    s_lat = jnp.einsum('bnqkgd,bnjkd->bnkgqj', q_blk, k_band).astype(jnp.float32) * scale
    s_lat = jnp.where(mask[None, :, None, None], s_lat, NEG_INF)
    s_ctx = jnp.einsum('bnqkgd,bckd->bnkgqc', q_blk, kz).astype(jnp.float32) * scale
    p = softmax_with_sink(jnp.concatenate([s_lat, s_ctx], axis=-1),
                          sink_f[None, None, :, :, None, None]).astype(v.dtype)
    n_lat = 3 * SWA_BLOCK
    o = (jnp.einsum('bnkgqj,bnjkd->bnqkgd', p[..., :n_lat], v_band)
         + jnp.einsum('bnkgqc,bckd->bnqkgd', p[..., n_lat:], vz))
    ox = o.reshape(B, S, H * dh) @ w_o
    oz = None
    if need_ctx:
        s = jnp.einsum('bqkgd,bckd->bkgqc', qz, kz).astype(jnp.float32) * scale
        pz = softmax_with_sink(s, sink_f[None, :, :, None, None]).astype(vz.dtype)
        oz = jnp.einsum('bkgqc,bckd->bqkgd', pz, vz).reshape(B, T, H * dh) @ w_o
    return ox, oz


def setup_inputs(seed: int = 0) -> dict:
    key = jax.random.key(seed)
    ks = jax.random.split(key, 24)
    D, F = D_MODEL, D_FF
    n_a = len(range(0, DEPTH, N_MIXERS))
    n_b = len(range(1, DEPTH, N_MIXERS))
    n_c = len(range(2, DEPTH, N_MIXERS))

    def nrm(k, shape, s):
        return jax.random.normal(k, shape, jnp.float32) * s

    def gain(k, shape):
        return 1.0 + 0.02 * jax.random.normal(k, shape, jnp.float32)

    mla_down = MLA_Q_LORA + MLA_KV_LORA + MLA_ROPE
    swa_qkv = (SWA_HEADS + 2 * SWA_KV_HEADS) * SWA_HEAD_DIM
    return {
        'x': nrm(ks[0], (BATCH, SEQ, D), 1.0),
        'c': nrm(ks[1], (BATCH, D), 1.0),
        'ctx': nrm(ks[2], (BATCH, CTX_LEN, D), 1.0),
        'c_ctx': nrm(ks[3], (D,), 1.0),
        'w_mod': nrm(ks[4], (DEPTH, D, N_MOD * D), 0.5 * D ** -0.5),
        'b_mod': nrm(ks[5], (DEPTH, N_MOD * D), 0.01),
        'norm_g': gain(ks[6], (DEPTH, 3, D)),
        'ffn_w_in': nrm(ks[7], (DEPTH, 2, D, 2 * F), D ** -0.5),
        'ffn_w_out': nrm(ks[8], (DEPTH, 2, F, D), F ** -0.5),
        'na_w_qkv': nrm(ks[9], (n_a, D, 3 * NA_HEADS * NA_HEAD_DIM), D ** -0.5),
        'na_rpb': nrm(ks[10], (n_a, NA_HEADS, 2 * NA_KH - 1, 2 * NA_KW - 1), 0.1),
        'na_w_o': nrm(ks[11], (n_a, NA_HEADS * NA_HEAD_DIM, D), (NA_HEADS * NA_HEAD_DIM) ** -0.5),
        'mla_w_down': nrm(ks[12], (n_b, D, mla_down), D ** -0.5),
        'mla_q_norm_g': gain(ks[13], (n_b, MLA_Q_LORA)),
        'mla_w_q_up': nrm(ks[14], (n_b, MLA_Q_LORA, MLA_HEADS * (MLA_NOPE + MLA_ROPE)), MLA_Q_LORA ** -0.5),
        'mla_kv_norm_g': gain(ks[15], (n_b, MLA_KV_LORA)),
        'mla_w_kv_up': nrm(ks[16], (n_b, MLA_KV_LORA, MLA_HEADS * (MLA_NOPE + MLA_V)), MLA_KV_LORA ** -0.5),
        'mla_w_o': nrm(ks[17], (n_b, MLA_HEADS * MLA_V, D), (MLA_HEADS * MLA_V) ** -0.5),
        'swa_w_qkv': nrm(ks[18], (n_c, D, swa_qkv), D ** -0.5),
        'swa_sink': nrm(ks[19], (n_c, SWA_HEADS), 0.5),
        'swa_w_o': nrm(ks[20], (n_c, SWA_HEADS * SWA_HEAD_DIM, D), (SWA_HEADS * SWA_HEAD_DIM) ** -0.5),
        'final_norm_g': gain(ks[21], (D,)),
    }


def reference(x, c, ctx, c_ctx, w_mod, b_mod, norm_g, ffn_w_in, ffn_w_out,
              na_w_qkv, na_rpb, na_w_o,
              mla_w_down, mla_q_norm_g, mla_w_q_up, mla_kv_norm_g, mla_w_kv_up, mla_w_o,
              swa_w_qkv, swa_sink, swa_w_o, final_norm_g):
    z = ctx
    for li in range(DEPTH):
        need_ctx = li < DEPTH - 1
        j = li // N_MIXERS
        mx = adaln_params(c[:, None, :], w_mod[li], b_mod[li])
        mz = adaln_params(c_ctx[None, None, :], w_mod[li], b_mod[li])

        x = x + 0.5 * mx[2] * swiglu(modulate(x, norm_g[li, 0], mx[0], mx[1]), ffn_w_in[li, 0], ffn_w_out[li, 0])
        z = z + 0.5 * mz[2] * swiglu(modulate(z, norm_g[li, 0], mz[0], mz[1]), ffn_w_in[li, 0], ffn_w_out[li, 0])

        hx = modulate(x, norm_g[li, 1], mx[3], mx[4])
        hz = modulate(z, norm_g[li, 1], mz[3], mz[4])
        kind = li % N_MIXERS
        if kind == 0:
            ox, oz = neighbourhood_attention(hx, hz, na_w_qkv[j], na_rpb[j], na_w_o[j], need_ctx)
        elif kind == 1:
            ox, oz = mla_attention(hx, hz, mla_w_down[j], mla_q_norm_g[j], mla_w_q_up[j],
                                   mla_kv_norm_g[j], mla_w_kv_up[j], mla_w_o[j], need_ctx)
        else:
            ox, oz = window_gqa_sink(hx, hz, swa_w_qkv[j], swa_sink[j], swa_w_o[j], need_ctx)
        x = x + mx[5] * ox

        x = x + 0.5 * mx[8] * swiglu(modulate(x, norm_g[li, 2], mx[6], mx[7]), ffn_w_in[li, 1], ffn_w_out[li, 1])
        if need_ctx:
            z = z + mz[5] * oz
            z = z + 0.5 * mz[8] * swiglu(modulate(z, norm_g[li, 2], mz[6], mz[7]), ffn_w_in[li, 1], ffn_w_out[li, 1])
    return rmsnorm(x, final_norm_g)
```

```python
import contextlib
import numpy as np
import concourse.bass as bass
import concourse.mybir as mybir
from concourse.bass_utils import run_bass_kernel_spmd

F32 = mybir.dt.float32
BF16 = mybir.dt.bfloat16
ALU = mybir.AluOpType
AF = mybir.ActivationFunctionType

D = 2048
KC = 16
NLAT = 2048
NCTX = 256
NTOK = NLAT + NCTX
FF = 5632
DEPTH = 4
EPS = 1e-6
NEG = -30000.0
TOK_TILES = [(0, 512), (512, 512), (1024, 512), (1536, 512), (2048, 256)]
ENGS = ("pe", "act", "dve", "pool", "sp")


class Res:
    __slots__ = ("name", "lw", "rd", "slot", "dcount", "lastd")

    def __init__(self, name):
        self.name = name
        self.lw = None
        self.rd = {}
        self.slot = None
        self.dcount = 0
        self.lastd = None


class Slot:
    __slots__ = ("count", "free", "sem")

    def __init__(self):
        self.count = 0
        self.free = True
        self.sem = None


class Prog:
    def __init__(self):
        self.ops = {e: [] for e in ENGS}
        self.known = {e: {} for e in ENGS}
        self.slots = []
        self.active = []
        self.ccres = []

    def _deps(self, eng, reads, writes, extra=()):
        deps = list(extra)
        for r in reads:
            if r.lw is not None:
                deps.append(r.lw)
        for w in writes:
            if w.lw is not None:
                deps.append(w.lw)
            deps.extend(w.rd.values())
        waits = []
        kn = self.known[eng]
        for d in deps:
            if d[0] == "e":
                _, E, i = d
                if E == eng and eng == "pe":
                    continue
                if kn.get(E, -1) >= i:
                    continue
                kn[E] = i
                self.ops[E][i][2] = True
                waits.append(d)
            else:
                _, res, v = d
                if kn.get(res, 0) >= v:
                    continue
                kn[res] = v
                waits.append(d)
        return waits

    def _post(self, ev, reads, writes):
        for r in reads:
            if ev[0] == "e":
                r.rd[ev[1]] = ev
            else:
                r.rd[("d", ev[1])] = ev
        for w in writes:
            w.lw = ev
            w.rd = {}

    def op(self, eng, fn, reads=(), writes=()):
        waits = self._deps(eng, reads, writes)
        idx = len(self.ops[eng])
        self.ops[eng].append([fn, waits, False, None])
        ev = ("e", eng, idx)
        self._post(ev, reads, writes)
        return ev

    def dma(self, eng, fn, semres, reads=(), writes=(), n=1, cc=False):
        extra = [semres.lastd] if semres.lastd is not None else []
        waits = self._deps(eng, reads, writes, extra)
        if semres.slot is None and cc:
            slot = Slot()
            slot.free = False
            self.slots.append(slot)
            semres.slot = slot
            semres.dcount = 0
            self.ccres.append(semres)
        if semres.slot is None:
            slot = None
            for sl in self.slots:
                if sl.free:
                    slot = sl
                    break
            if slot is None:
                slot = Slot()
                self.slots.append(slot)
            slot.free = False
            semres.slot = slot
            semres.dcount = slot.count
            self.active.append(semres)
        semres.dcount += (1 if cc else 16 * n)
        semres.slot.count = semres.dcount
        ev = ("d", semres.slot, semres.dcount)
        semres.lastd = ev
        self.ops[eng].append([fn, waits, cc, semres.slot])
        self._post(ev, reads, writes)
        return ev

    def barrier(self):
        evs = []
        for E in ENGS:
            for i in range(len(self.ops[E]) - 1, -1, -1):
                o = self.ops[E][i]
                if o[0] is not None and o[3] is None:
                    evs.append(("e", E, i))
                    break
        for r in self.active + self.ccres:
            if r.lastd is not None:
                evs.append(r.lastd)
        for F in ENGS:
            waits = self._deps(F, (), (), [e for e in evs if not (e[0] == "e" and e[1] == F)])
            if waits:
                self.ops[F].append([None, waits, False, None])
        for r in self.active:
            r.slot.free = True
            r.slot = None
            r.lastd = None
        self.active = []

    def wait_all(self, eng, evs):
        waits = self._deps(eng, (), (), evs)
        self.ops[eng].append([None, waits, False, None])

    def emit(self, nc, stack):
        sems = {E: stack.enter_context(nc.semaphore("s_" + E)) for E in ENGS}
        for i, r in enumerate(self.slots):
            r.sem = stack.enter_context(nc.semaphore("d%d" % i))
        msval = {}
        for E in ENGS:
            c = 0
            vals = []
            for o in self.ops[E]:
                if o[2]:
                    c += 1
                vals.append(c)
            msval[E] = vals
        block = stack.enter_context(nc.Block())

        def run(E, e):
            for fn, waits, ms, semres in self.ops[E]:
                for w in waits:
                    if w[0] == "e":
                        e.wait_ge(sems[w[1]], msval[w[1]][w[2]])
                    else:
                        e.wait_ge(w[1].sem, w[2])
                if fn is None:
                    continue
                ins = fn(e)
                if semres is not None and ms:
                    ins.then_inc(semres.sem)
                elif semres is not None:
                    if isinstance(ins, (list, tuple)):
                        for i_ in ins:
                            i_.then_inc(semres.sem, 16)
                    else:
                        ins.then_inc(semres.sem, 16)
                elif ms:
                    ins.then_inc(sems[E], 1)

        block.tensor(lambda e: run("pe", e))
        block.scalar(lambda e: run("act", e))
        block.vector(lambda e: run("dve", e))
        block.gpsimd(lambda e: run("pool", e))
        block.sync(lambda e: run("sp", e))


class Ring:
    def __init__(self, tiles, name):
        self.tiles = tiles
        self.res = [Res("%s%d" % (name, i)) for i in range(len(tiles))]
        self.i = 0

    def next(self):
        k = self.i % len(self.tiles)
        self.i += 1
        return self.tiles[k], self.res[k]


class Arena:
    def __init__(self, tensor, nbytes):
        self.t = tensor
        self.n = nbytes
        self.off = 0

    def mark(self):
        return self.off

    def release(self, m):
        self.off = m

    def alloc(self, shape, dt, parts=128):
        nel = int(np.prod(shape))
        nb = nel * (4 if dt == F32 else 2)
        nb_al = (nb + 63) // 64 * 64
        assert self.off + nb_al <= self.n, ("arena overflow", self.off, nb_al, self.n)
        v = self.t[0:parts, self.off // 2:(self.off + nb) // 2]
        self.off += nb_al
        if dt == F32:
            v = v.bitcast(F32)
        if len(shape) == 2:
            v = v.rearrange("p (a b) -> p a b", a=shape[0])
        elif len(shape) == 3:
            v = v.rearrange("p (a b c) -> p a b c", a=shape[0], b=shape[1])
        elif len(shape) == 4:
            v = v.rearrange("p (a b c d) -> p a b c d", a=shape[0], b=shape[1], c=shape[2])
        return v


class Builder:
    def __init__(self, layers, first, last):
        self.layers = layers
        self.first = first
        self.last = last
        self.nc = bass.Bass("TRN2", target_bir_lowering=False)
        self.P = Prog()
        self.stack = contextlib.ExitStack()
        self.dram = {}

    def din(self, name, shape, dt=F32):
        t = self.nc.dram_tensor(name, list(shape), dt, kind="ExternalInput")
        self.dram[name] = t
        return t.ap()

    def dout(self, name, shape, dt=F32):
        t = self.nc.dram_tensor(name, list(shape), dt, kind="ExternalOutput")
        self.dram[name] = t
        return t.ap()

    def dtmp(self, name, shape, dt=F32):
        t = self.nc.dram_tensor(name, list(shape), dt)
        self.dram[name] = t
        return t

    def MM(self, b, out, lhsT, rhs, start, stop, reads):
        return self.P.op("pe", lambda e: e.matmul(out, lhsT=lhsT, rhs=rhs, start=start, stop=stop), reads=reads, writes=[self.bres[b]])

    def ACT(self, out, in_, func, reads, writes, bias=0.0, scale=1.0):
        return self.P.op("act", lambda e: e.activation(out=out, in_=in_, func=func, bias=bias, scale=scale), reads=reads, writes=writes)

    def TT(self, out, in0, in1, op, reads, writes, eng="dve"):
        return self.P.op(eng, lambda e: e.tensor_tensor(out=out, in0=in0, in1=in1, op=op), reads=reads, writes=writes)

    def STT(self, out, in0, scalar, in1, op0, op1, reads, writes, eng="dve"):
        return self.P.op(eng, lambda e: e.scalar_tensor_tensor(out=out, in0=in0, scalar=scalar, in1=in1, op0=op0, op1=op1), reads=reads, writes=writes)

    def TS(self, out, in0, s1, s2, op0, op1, reads, writes, eng="dve"):
        if s2 is None:
            return self.P.op(eng, lambda e: e.tensor_scalar(out=out, in0=in0, scalar1=s1, scalar2=None, op0=op0), reads=reads, writes=writes)
        return self.P.op(eng, lambda e: e.tensor_scalar(out=out, in0=in0, scalar1=s1, scalar2=s2, op0=op0, op1=op1), reads=reads, writes=writes)

    def CP(self, out, in_, reads, writes, eng="dve"):
        return self.P.op(eng, lambda e: e.tensor_copy(out=out, in_=in_), reads=reads, writes=writes)

    def RECIP(self, out, in_, reads, writes):
        return self.P.op("dve", lambda e: e.reciprocal(out=out, in_=in_), reads=reads, writes=writes)

    def DMA(self, eng, out, in_, semres, reads=(), writes=()):
        return self.P.dma(eng, lambda e: e.dma_start(out=out, in_=in_), semres, reads=reads, writes=writes)

    def DMAS(self, eng, pairs, semres, reads=(), writes=()):
        pairs = list(pairs)
        return self.P.dma(eng, lambda e: [e.dma_start(out=o, in_=i) for (o, i) in pairs], semres, reads=reads, writes=writes, n=len(pairs))

    def GATHER(self, src_t, dst_t, semres, reads, writes):
        rg = [[2 * i, 2 * i + 1] for i in range(NCORES // 2)]
        return self.P.dma("pool", lambda e: e.collective_compute("AllGather", ALU.bypass, replica_groups=rg,
                                                                 ins=[src_t.ap().opt()], outs=[dst_t.ap().opt()]),
                          semres, reads=reads, writes=writes, cc=True)

    def build(self):
        nc, P, st = self.nc, self.P, self.stack
        L = self.layers
        nL = len(L)
        import os
        self.ph = os.environ.get("KD", "mod,ffn1,mix,ffn2").split(",")
        self.xT_in = self.din("xT_in", [D, NTOK])
        self.csT = self.din("csT", [128, KC * 2])
        self.w_mod = self.din("w_mod", [nL, D, 9 * D])
        self.b_modT = self.din("b_modT", [nL, 128, 144])
        self.norm_gT = self.din("norm_gT", [nL, 128, 3 * KC])
        self.w_in = self.din("ffn_w_in", [nL, 2, D, 2 * FF])
        self.w_out = self.din("ffn_w_out", [nL, 2, FF, D])
        kinds = [li % 3 for li in L]
        self.n_na = kinds.count(0)
        if self.n_na:
            self.na_w_qkv = self.din("na_w_qkv", [self.n_na, D, 3 * D])
            self.na_w_o = self.din("na_w_o", [self.n_na, D, D])
            self.na_tab = self.din("na_tab", [self.n_na, 16, 128, 5 * 6 * 128])
        if 1 in kinds:
            self.mla_w_down = self.din("mla_w_down", [D, 1088])
            self.mla_w_q_up = self.din("mla_w_q_up", [512, 3072])
            self.mla_w_kv_up = self.din("mla_w_kv_up", [512, 4096])
            self.mla_w_o = self.din("mla_w_o", [D, D])
            self.mla_gT = self.din("mla_gT", [128, 8])
            self.rope64 = self.din("rope64", [64, 2 * NLAT])
            self.piT64 = self.din("piT64", [128, 128])
        if 2 in kinds:
            self.swa_w_qkv = self.din("swa_w_qkv", [D, 3072])
            self.swa_w_o = self.din("swa_w_o", [D, D])
            self.swa_mask = self.din("swa_mask", [128, 5 * 4 * 128])
            self.swa_sink = self.din("swa_sink", [128, 16])
            self.rope128 = self.din("rope128", [128, 2 * NLAT])
            self.piT128 = self.din("piT128", [128, 128])
        if self.last:
            self.final_gT = self.din("final_gT", [128, KC])
            self.yT = self.dout("yT", [D, NLAT])
        else:
            self.xT_out = self.dout("xT_out", [D, NTOK])
        self.dbg = bool(os.environ.get("KDBG"))
        if self.dbg:
            self.dbg_QK = self.dout("dbg_QK", [2 * D, NTOK], BF16)
            self.dbg_VD = self.dout("dbg_VD", [NTOK, D], BF16)
            self.dbg_oT = self.dout("dbg_oT", [D, NTOK], BF16)
        self.XT = self.dtmp("XT", [D, NTOK]).ap()
        self.XTv = self.XT.rearrange("(c p) t -> p c t", p=128)
        self.xres = [[Res("x%d_%d" % (c, t)) for t in range(NTOK // 128)] for c in range(KC)]
        self.QK = self.dtmp("QK", [2 * D, NTOK], BF16).ap()
        self.VD = self.dtmp("VD", [NTOK, D], BF16).ap()
        self.r_QK = Res("QK")
        self.r_VD = Res("VD")
        self.EXK_t = self.dtmp("EXK", [2 * D, 256], BF16)
        self.EXV_t = self.dtmp("EXV", [2 * 256, D], BF16)
        self.GK_t = self.dtmp("GK", [4 * D, 256], BF16)
        self.GV_t = self.dtmp("GV", [4 * 256, D], BF16)
        self.EXL_t = self.dtmp("EXL", [1024, NLAT], BF16)
        self.GL_t = self.dtmp("GL", [2 * 1024, NLAT], BF16)
        self.EXK2_t = self.dtmp("EXK2", [2 * 512, 128], BF16)
        self.EXV2_t = self.dtmp("EXV2", [2 * 128, 512], BF16)
        self.GK2_t = self.dtmp("GK2", [4 * 512, 128], BF16)
        self.GV2_t = self.dtmp("GV2", [4 * 128, 512], BF16)
        self.CQ = self.dtmp("CQ", [512, NTOK], BF16).ap()
        self.CKC = self.dtmp("CKC", [576, NCTX], BF16).ap()
        self.KA = self.dtmp("KA", [D, 4352], BF16).ap()
        self.VA = self.dtmp("VA", [4352, D], BF16).ap()
        self.r_EX = Res("EX")
        self.r_G = Res("G")

        ARENA_BYTES = 204 * 1024
        arena_t = st.enter_context(nc.sbuf_tensor("arena", [128, ARENA_BYTES // 2], BF16))
        self.A = Arena(arena_t, ARENA_BYTES)
        self.pst = [st.enter_context(nc.psum_tensor("ps%d" % i, [128, 1024], F32)) for i in range(4)]
        self.banks = [self.pst[i // 2][:, (i % 2) * 512:(i % 2) * 512 + 512] for i in range(8)]
        self.bres = [Res("bank%d" % i) for i in range(8)]
        A = self.A
        self.ones_bf = A.alloc([128], BF16)
        self.inv2048_bf = A.alloc([128], BF16)
        self.inv512_bf = A.alloc([128], BF16)
        self.modv = A.alloc([nL, 9, KC, 2], F32)
        self.gT = A.alloc([nL, 3, KC], F32)
        self.silu_c = A.alloc([KC, 2], F32)
        self.cA = A.alloc([KC, 2], F32)
        self.cG = A.alloc([KC, 2], F32)
        self.r_const = Res("const")
        self.r_modv = Res("modv")
        self.r_AG = Res("AG")
        P.op("dve", lambda e: e.memset(self.ones_bf, 1.0), writes=[self.r_const])
        P.op("dve", lambda e: e.memset(self.inv2048_bf, 1.0 / 2048), writes=[self.r_const])
        P.op("dve", lambda e: e.memset(self.inv512_bf, 1.0 / 512), writes=[self.r_const])

        r_cp = Res("cp")
        for c in range(KC):
            self.DMA("sp", self.XT[c * 128:(c + 1) * 128, :], self.xT_in[c * 128:(c + 1) * 128, :], r_cp, writes=self.xres[c])
        self.mod_phase()
        P.barrier()
        ina = 0
        for ll, li in enumerate(L):
            need_ctx = li < DEPTH - 1
            kind = li % 3
            if "ffn1" in self.ph:
                self.ffn(ll, 0, True)
                P.barrier()
            if "mix" in self.ph:
                if kind == 0:
                    self.mixer_na(ll, ina, need_ctx)
                elif kind == 1:
                    self.mixer_mla(ll, need_ctx)
                else:
                    self.mixer_swa(ll, need_ctx)
                P.barrier()
            if kind == 0:
                ina += 1
            if "ffn2" in self.ph:
                self.ffn(ll, 1, need_ctx)
                P.barrier()
        evs = []
        r_out = Res("outcp")
        if self.last:
            self.final_norm()
            evs = self.final_evs
        else:
            for c in range(KC):
                evs.append(self.DMA("sp", self.xT_out[c * 128:(c + 1) * 128, :], self.XT[c * 128:(c + 1) * 128, :], r_out, reads=self.xres[c]))
        P.wait_all("sp", evs)
        P.emit(nc, st)
        st.close()
        return nc

    def mod_phase(self):
        nc, P, A = self.nc, self.P, self.A
        nL = len(self.layers)
        m0 = A.mark()
        csT_sb = A.alloc([KC, 2], F32)
        bm = A.alloc([nL, 144], F32)
        wring = Ring([A.alloc([KC, 256], F32) for _ in range(3)], "wm")
        r_in = Res("modin")
        P.dma("sp", lambda e: e.dma_start(out=csT_sb.rearrange("p a b -> p (a b)"), in_=self.csT[:, :]), r_in, writes=[r_in])
        r_bm = Res("bm")
        P.dma("sp", lambda e: e.dma_start(out=bm, in_=self.b_modT.rearrange("l p c -> p l c")), r_bm, writes=[r_bm])
        r_g = Res("gT")
        P.dma("sp", lambda e: e.dma_start(out=self.gT.rearrange("p l a b -> p l (a b)"),
                                          in_=self.norm_gT.rearrange("l p c -> p l c")), r_g, writes=[self.r_const])
        P.op("act", lambda e: e.activation(out=self.silu_c, in_=csT_sb, func=AF.Silu), reads=[r_in], writes=[self.r_const])
        bank_i = 0
        for l in range(nL):
            wv = self.w_mod[l].rearrange("(kc p) n -> p kc n", p=128)
            for blk in range(72):
                wt, wr = wring.next()
                P.dma("sp", (lambda wt, blk, wv: lambda e: e.dma_start(out=wt, in_=wv[:, :, blk * 256:(blk + 1) * 256]))(wt, blk, wv),
                      wr, writes=[wr])
                b = 6 + (bank_i % 2)
                bank_i += 1
                ps = self.banks[b]
                for h in range(2):
                    ch = blk * 2 + h
                    for kc in range(KC):
                        P.op("pe", (lambda ps, wt, h, kc: lambda e: e.matmul(
                            ps[:, h * 2:h * 2 + 2], lhsT=wt[:, kc, h * 128:(h + 1) * 128], rhs=self.silu_c[:, kc, :],
                            start=(kc == 0), stop=(kc == KC - 1)))(ps, wt, h, kc),
                            reads=[wr, self.r_const], writes=[self.bres[b]])
                for h in range(2):
                    ch = blk * 2 + h
                    k, c = ch // KC, ch % KC
                    P.op("dve", (lambda ps, h, l, k, c, ch: lambda e: e.tensor_scalar(
                        out=self.modv[:, l, k, c, :], in0=ps[:, h * 2:h * 2 + 2], scalar1=bm[:, l, ch:ch + 1], scalar2=None,
                        op0=ALU.add))(ps, h, l, k, c, ch),
                        reads=[self.bres[b], r_bm], writes=[self.r_modv])
        self.P.barrier()
        A.release(m0)

    def set_mod(self, ll, s, gate_mul):
        P = self.P
        for stm in range(2):
            P.op("dve", (lambda stm: lambda e: e.scalar_tensor_tensor(
                out=self.cA[:, :, stm], in0=self.modv[:, ll, 3 * s + 1, :, stm], scalar=1.0, in1=self.gT[:, ll, s, :],
                op0=ALU.add, op1=ALU.mult))(stm), reads=[self.r_modv, self.r_const], writes=[self.r_AG])
        P.op("dve", lambda e: e.tensor_scalar(out=self.cG, in0=self.modv[:, ll, 3 * s + 2, :, :], scalar1=float(gate_mul),
                                              scalar2=None, op0=ALU.mult), reads=[self.r_modv], writes=[self.r_AG])

    def norm_phase(self, ll, s, hT, hres, nt128=NTOK // 128):
        nc, P, A = self.nc, self.P, self.A
        xs_ring = Ring([A.alloc([KC, 128], F32) for _ in range(2)], "xs")
        sq_ring = Ring([A.alloc([KC, 128], BF16) for _ in range(2)], "sq")
        rs_ring = Ring([A.alloc([128], F32) for _ in range(2)], "rs")
        for t in range(nt128):
            stm = 0 if t < NLAT // 128 else 1
            t0 = t * 128
            xs, xr = xs_ring.next()
            sq, sr = sq_ring.next()
            rs, rr = rs_ring.next()
            b = 6 + (t % 2)
            ps = self.banks[b]
            P.dma("sp", (lambda xs, t0: lambda e: e.dma_start(out=xs, in_=self.XTv[:, :, t0:t0 + 128]))(xs, t0), xr,
                  reads=[self.xres[c][t] for c in range(KC)], writes=[xr])
            P.op("act", (lambda sq, xs: lambda e: e.activation(out=sq, in_=xs, func=AF.Square))(sq, xs), reads=[xr], writes=[sr])
            for c in range(KC):
                P.op("pe", (lambda ps, sq, c: lambda e: e.matmul(ps[:, 0:128], lhsT=self.inv2048_bf, rhs=sq[:, c, :],
                                                                 start=(c == 0), stop=(c == KC - 1)))(ps, sq, c),
                     reads=[sr, self.r_const], writes=[self.bres[b]])
            P.op("act", (lambda rs, ps: lambda e: e.activation(out=rs, in_=ps[:, 0:128], func=AF.Sqrt, bias=EPS, scale=1.0))(rs, ps),
                 reads=[self.bres[b]], writes=[rr])
            P.op("dve", (lambda rs: lambda e: e.reciprocal(out=rs, in_=rs))(rs), reads=[rr], writes=[rr])
            for c in range(KC):
                P.op("dve", (lambda xs, rs, c, stm: lambda e: e.scalar_tensor_tensor(
                    out=xs[:, c, :], in0=xs[:, c, :], scalar=self.cA[:, c, stm:stm + 1], in1=rs,
                    op0=ALU.mult, op1=ALU.mult))(xs, rs, c, stm), reads=[rr, xr, self.r_AG], writes=[xr])
            for c in range(KC):
                P.op("act", (lambda xs, c, stm, t0: lambda e: e.activation(
                    out=hT[:, c, t0:t0 + 128], in_=xs[:, c, :], func=AF.Identity,
                    bias=self.modv[:, ll, 3 * s, c, stm:stm + 1], scale=1.0))(xs, c, stm, t0),
                    reads=[xr, self.r_modv], writes=[hres[t]])

    class Resid:
        def __init__(self, B, targets, nslots=4, look=2, ring=None):
            self.B = B
            self.targets = targets
            self.ring = ring if ring is not None else Ring([B.A.alloc([512], F32) for _ in range(nslots)], "xr")
            self.look = look
            self.loaded = {}
            self.nl = 0
            self.k = 0

        def _load(self):
            B = self.B
            if self.nl >= len(self.targets):
                return
            c, ti = self.targets[self.nl]
            t0, n = TOK_TILES[ti]
            xt, xr = self.ring.next()
            rl = [B.xres[c][t] for t in range(t0 // 128, (t0 + n) // 128)]
            B.P.dma("sp", lambda e: e.dma_start(out=xt[:, 0:n], in_=B.XT[c * 128:(c + 1) * 128, t0:t0 + n]), xr,
                    reads=rl, writes=[xr])
            self.loaded[self.nl] = (xt, xr, rl)
            self.nl += 1

        def apply(self, ps_ap, ps_res):
            B = self.B
            while self.nl <= min(self.k + self.look, len(self.targets) - 1):
                self._load()
            c, ti = self.targets[self.k]
            t0, n = TOK_TILES[ti]
            stm = 0 if ti < 4 else 1
            xt, xr, rl = self.loaded.pop(self.k)
            B.P.op("dve", lambda e: e.scalar_tensor_tensor(out=xt[:, 0:n], in0=ps_ap, scalar=B.cG[:, c, stm:stm + 1],
                                                            in1=xt[:, 0:n], op0=ALU.mult, op1=ALU.add),
                   reads=[ps_res, xr, B.r_AG], writes=[xr])
            B.P.dma("sp", lambda e: e.dma_start(out=B.XT[c * 128:(c + 1) * 128, t0:t0 + n], in_=xt[:, 0:n]), xr,
                    reads=[xr], writes=rl)
            self.k += 1

    def ffn(self, ll, f, with_ctx=True):
        nc, P, A = self.nc, self.P, self.A
        s = 0 if f == 0 else 2
        self.set_mod(ll, s, 0.5)
        m0 = A.mark()
        hT = A.alloc([KC, NTOK], BF16)
        hres = [Res("h%d" % t) for t in range(NTOK // 128)]
        TILES = TOK_TILES if with_ctx else TOK_TILES[:4]
        self.norm_phase(ll, s, hT, hres, (NTOK if with_ctx else NLAT) // 128)
        GB = 4
        actT = A.alloc([2 * GB, NTOK], BF16)
        ares = [[Res("a%d_%d" % (c, t)) for t in range(len(TOK_TILES))] for c in range(2 * GB)]
        wg_ring = Ring([A.alloc([KC, 256], BF16) for _ in range(2)], "wg")
        wu_ring = Ring([A.alloc([KC, 256], BF16) for _ in range(2)], "wu")
        wo_ring = Ring([A.alloc([2 * GB, 256], BF16) for _ in range(2)], "wo")
        sg_ring = Ring([A.alloc([512], F32) for _ in range(2)], "sg")
        xr_ring = Ring([A.alloc([512], F32) for _ in range(4)], "xr")
        wiv = self.w_in[ll, f].rearrange("(kc p) n -> p kc n", p=128)
        nblk = FF // 256
        groups = [list(range(g, min(g + GB, nblk))) for g in range(0, nblk, GB)]
        it = 0
        ob = 0
        for grp in groups:
            for bi, blk in enumerate(grp):
                wg, wgr = wg_ring.next()
                wu, wur = wu_ring.next()
                P.dma("pool", (lambda wg, blk: lambda e: e.dma_start(out=wg, in_=wiv[:, :, blk * 256:(blk + 1) * 256]))(wg, blk),
                      wgr, writes=[wgr])
                P.dma("pool", (lambda wu, blk: lambda e: e.dma_start(out=wu, in_=wiv[:, :, FF + blk * 256:FF + (blk + 1) * 256]))(wu, blk),
                      wur, writes=[wur])
                for ti, (t0, n) in enumerate(TILES):
                    hr = hres[t0 // 128:(t0 + n) // 128]
                    for h in range(2):
                        ac = bi * 2 + h
                        bg = (it % 2) * 2
                        bu = bg + 1
                        it += 1
                        psg, psu = self.banks[bg], self.banks[bu]
                        for kc in range(KC):
                            P.op("pe", (lambda psg, wg, h, kc, t0, n: lambda e: e.matmul(
                                psg[:, 0:n], lhsT=wg[:, kc, h * 128:(h + 1) * 128], rhs=hT[:, kc, t0:t0 + n],
                                start=(kc == 0), stop=(kc == KC - 1)))(psg, wg, h, kc, t0, n),
                                reads=[wgr] + hr, writes=[self.bres[bg]])
                        for kc in range(KC):
                            P.op("pe", (lambda psu, wu, h, kc, t0, n: lambda e: e.matmul(
                                psu[:, 0:n], lhsT=wu[:, kc, h * 128:(h + 1) * 128], rhs=hT[:, kc, t0:t0 + n],
                                start=(kc == 0), stop=(kc == KC - 1)))(psu, wu, h, kc, t0, n),
                                reads=[wur] + hr, writes=[self.bres[bu]])
                        sg, sgr = sg_ring.next()
                        P.op("act", (lambda sg, psg, n: lambda e: e.activation(out=sg[:, 0:n], in_=psg[:, 0:n], func=AF.Silu))(sg, psg, n),
                             reads=[self.bres[bg]], writes=[sgr])
                        P.op("dve", (lambda sg, psu, ac, t0, n: lambda e: e.tensor_tensor(
                            out=actT[:, ac, t0:t0 + n], in0=sg[:, 0:n], in1=psu[:, 0:n], op=ALU.mult))(sg, psu, ac, t0, n),
                            reads=[sgr, self.bres[bu]], writes=[ares[ac][ti]])
            nch = 2 * len(grp)
            r0 = grp[0] * 256
            wov = self.w_out[ll, f, r0:r0 + nch * 128, :].rearrange("(kc p) n -> p kc n", p=128)
            m1 = A.mark()
            targets = [(db * 2 + h, ti) for db in range(D // 256) for ti in range(len(TILES)) for h in range(2)]
            rs = Builder.Resid(self, targets, ring=xr_ring)
            for db in range(D // 256):
                wo, wor = wo_ring.next()
                P.dma("pool", (lambda wo, db, wov, nch: lambda e: e.dma_start(out=wo[:, 0:nch, :], in_=wov[:, :, db * 256:(db + 1) * 256]))(wo, db, wov, nch),
                      wor, writes=[wor])
                for ti, (t0, n) in enumerate(TILES):
                    for h in range(2):
                        b = 4 + (ob % 2)
                        ob += 1
                        ps = self.banks[b]
                        for a in range(nch):
                            P.op("pe", (lambda ps, wo, a, h, t0, n, nch: lambda e: e.matmul(
                                ps[:, 0:n], lhsT=wo[:, a, h * 128:(h + 1) * 128], rhs=actT[:, a, t0:t0 + n],
                                start=(a == 0), stop=(a == nch - 1)))(ps, wo, a, h, t0, n, nch),
                                reads=[wor, ares[a][ti]], writes=[self.bres[b]])
                        rs.apply(ps[:, 0:n], self.bres[b])
            A.release(m1)
        A.release(m0)


    def linear_fm(self, w2d, kcn, in_buf, in_res, cols, tiles, evac, banks=(0, 1, 2, 3), blkw=256):
        P, A = self.P, self.A
        m0 = A.mark()
        wring = Ring([A.alloc([kcn, blkw], BF16) for _ in range(3)], "wl")
        wv = w2d.rearrange("(kc p) n -> p kc n", p=128)
        blocks = []
        for ci, (c0, w) in enumerate(cols):
            if blocks and blocks[-1][0] + blocks[-1][1] == c0 and blocks[-1][1] + w <= blkw:
                blocks[-1][1] += w
                blocks[-1][2].append((ci, c0, w))
            else:
                blocks.append([c0, w, [(ci, c0, w)]])
        it = 0
        for b0, bw, chunks in blocks:
            wt, wr = wring.next()
            self.DMA("pool", wt[:, :, 0:bw], wv[:, :, b0:b0 + bw], wr, writes=[wr])
            for ti, (t0, n) in enumerate(tiles):
                rr = in_res(t0, n)
                for (ci, c0, w) in chunks:
                    b = banks[it % len(banks)]
                    it += 1
                    ps = self.banks[b]
                    for kc in range(kcn):
                        self.MM(b, ps[0:w, 0:n], wt[:, kc, c0 - b0:c0 - b0 + w], in_buf[:, kc, t0:t0 + n], kc == 0, kc == kcn - 1, [wr] + rr)
                    evac(ci, ti, ps[0:w, 0:n], b)
        self.P.barrier()
        A.release(m0)

    def linear_tm(self, w2d, kcn, in_buf, in_res, c0, ncols, ntok128, evac, banks=(0, 1, 2, 3)):
        P, A = self.P, self.A
        m0 = A.mark()
        wring = Ring([A.alloc([kcn, 256], BF16) for _ in range(3)], "wt")
        wv = w2d.rearrange("(kc p) n -> p kc n", p=128)
        it = 0
        for b0 in range(c0, c0 + ncols, 256):
            wt, wr = wring.next()
            self.DMA("pool", wt, wv[:, :, b0:b0 + 256], wr, writes=[wr])
            for t in range(ntok128):
                b = banks[it % len(banks)]
                it += 1
                ps = self.banks[b]
                rr = in_res(t * 128, 128)
                for kc in range(kcn):
                    self.MM(b, ps[:, 0:256], in_buf[:, kc, t * 128:(t + 1) * 128], wt[:, kc, :], kc == 0, kc == kcn - 1, [wr] + rr)
                evac(t, b0 - c0, ps[:, 0:256], b)
        self.P.barrier()
        A.release(m0)

    def final_norm(self):
        P, A = self.P, self.A
        m0 = A.mark()
        fg = A.alloc([KC], F32)
        r_fg = Res("fg")
        self.DMA("sp", fg, self.final_gT, r_fg, writes=[r_fg])
        xs_ring = Ring([A.alloc([KC, 128], F32) for _ in range(3)], "fxs")
        sq_ring = Ring([A.alloc([KC, 128], BF16) for _ in range(2)], "fsq")
        rs_ring = Ring([A.alloc([128], F32) for _ in range(2)], "frs")
        yv = self.yT.rearrange("(c p) t -> p c t", p=128)
        self.final_evs = []
        for t in range(NLAT // 128):
            t0 = t * 128
            xs, xr = xs_ring.next()
            sq, sr = sq_ring.next()
            rs, rr = rs_ring.next()
            b = 6 + (t % 2)
            ps = self.banks[b]
            self.DMA("sp", xs, self.XTv[:, :, t0:t0 + 128], xr, reads=[self.xres[c][t] for c in range(KC)], writes=[xr])
            self.ACT(sq, xs, AF.Square, [xr], [sr])
            for c in range(KC):
                self.MM(b, ps[:, 0:128], self.inv2048_bf, sq[:, c, :], c == 0, c == KC - 1, [sr, self.r_const])
            self.ACT(rs, ps[:, 0:128], AF.Sqrt, [self.bres[b]], [rr], bias=EPS)
            self.RECIP(rs, rs, [rr], [rr])
            for c in range(KC):
                self.STT(xs[:, c, :], xs[:, c, :], fg[:, c:c + 1], rs, ALU.mult, ALU.mult, [rr, xr, r_fg], [xr])
            self.final_evs.append(self.DMA("sp", yv[:, :, t0:t0 + 128], xs, xr, reads=[xr]))
        A.release(m0)

    def mixer_pre(self, ll):
        self.set_mod(ll, 1, 1.0)
        A = self.A
        hT = A.alloc([KC, NTOK], BF16)
        hres = [Res("mh%d" % t) for t in range(NTOK // 128)]
        self.norm_phase(ll, 1, hT, hres)
        return hT, hres

    def mixer_out(self, w_o, oT, ores, need_ctx):
        tiles = TOK_TILES if need_ctx else TOK_TILES[:4]
        if self.dbg:
            r_d = Res("dbg")
            self.DMA("sp", self.dbg_QK, self.QK, r_d, reads=[self.r_QK])
            self.DMA("sp", self.dbg_VD, self.VD, r_d, reads=[self.r_VD])
            self.DMA("sp", self.dbg_oT.rearrange("(c p) t -> p c t", p=128), oT, r_d, reads=ores)
            self.P.barrier()
        m1 = self.A.mark()
        targets = []
        for blk in range(D // 256):
            for ti in range(len(tiles)):
                for h in range(2):
                    targets.append((blk * 2 + h, ti))
        rs = Builder.Resid(self, targets)
        self.linear_fm(w_o, KC, oT, lambda t0, n: ores, [(c * 128, 128) for c in range(KC)], tiles,
                       lambda ci, ti, ps, b: rs.apply(ps, self.bres[b]), banks=(4, 5))
        self.A.release(m1)

    def mixer_na(self, ll, j, need_ctx):
        P, A = self.P, self.A
        m0 = A.mark()
        hT, hres = self.mixer_pre(ll)
        hr = lambda t0, n: hres[t0 // 128:(t0 + n) // 128]
        wq = self.na_w_qkv[j]
        st_ring = Ring([A.alloc([512], BF16) for _ in range(4)], "st")

        def ev_qk(ci, ti, ps, b):
            t0, n = TOK_TILES[ti]
            stg, sr = st_ring.next()
            if ci % 2 == 0:
                self.ACT(stg[:, 0:n], ps, AF.Identity, [self.bres[b]], [sr])
            else:
                self.CP(stg[:, 0:n], ps, [self.bres[b]], [sr])
            self.DMA("sp", self.QK[ci * 128:(ci + 1) * 128, t0:t0 + n], stg[:, 0:n], sr, reads=[sr], writes=[self.r_QK])
        self.linear_fm(wq, KC, hT, hr, [(c * 128, 128) for c in range(32)], TOK_TILES, ev_qk)

        def ev_v(t, cb, ps, b):
            stg, sr = st_ring.next()
            if t % 2 == 0:
                self.ACT(stg[:, 0:256], ps, AF.Identity, [self.bres[b]], [sr])
            else:
                self.CP(stg[:, 0:256], ps, [self.bres[b]], [sr])
            self.DMA("sp", self.VD[t * 128:(t + 1) * 128, cb:cb + 256], stg[:, 0:256], sr, reads=[sr], writes=[self.r_VD])
        self.linear_tm(wq, KC, hT, hr, 2 * D, D, NTOK // 128, ev_v)
        P.barrier()
        A.release(m0)
        m0 = A.mark()
        r_x = Res("nax")
        EXK, EXV, GK, GV = self.EXK_t.ap(), self.EXV_t.ap(), self.GK_t.ap(), self.GV_t.ap()
        for e_, tk in enumerate((0, NLAT - 256)):
            self.DMA("sp", EXK[e_ * D:(e_ + 1) * D, :], self.QK[D:2 * D, tk:tk + 256], r_x, reads=[self.r_QK], writes=[self.r_EX])
            self.DMA("sp", EXV[e_ * 256:(e_ + 1) * 256, :], self.VD[tk:tk + 256, :], r_x, reads=[self.r_VD], writes=[self.r_EX])
        r_cc = Res("nacc")
        self.GATHER(self.EXK_t, self.GK_t, r_cc, [self.r_EX], [self.r_G])
        self.GATHER(self.EXV_t, self.GV_t, r_cc, [self.r_EX], [self.r_G])
        P.barrier()
        oT = A.alloc([KC, NTOK], BF16)
        ores = [Res("o%d" % h_) for h_ in range(16)]
        scale = 128.0 ** -0.5
        NB = 2
        kx_ring = Ring([A.alloc([2816], BF16) for _ in range(NB)], "kx")
        vx_ring = Ring([A.alloc([22, 128], BF16) for _ in range(NB)], "vx")
        q_ring = Ring([A.alloc([NTOK], BF16) for _ in range(NB)], "qh")
        tb_ring = Ring([A.alloc([5, 6, 128], F32) for _ in range(NB)], "tb")
        sb_ring = Ring([A.alloc([768], F32) for _ in range(2)], "sb")
        pt_ring = Ring([A.alloc([1024], BF16) for _ in range(2)], "pt")
        ri_ring = Ring([A.alloc([128], F32) for _ in range(2)], "ri")
        it = 0
        for h in range(16):
            kx, kr = kx_ring.next()
            vx, vr = vx_ring.next()
            qh, qr = q_ring.next()
            tb, tr = tb_ring.next()
            r0 = D + h * 128
            self.DMAS("sp", [
                (kx[:, 0:256], GK[D + h * 128:D + (h + 1) * 128, :]),
                (kx[:, 256:2304], self.QK[r0:r0 + 128, 0:NLAT]),
                (kx[:, 2304:2560], GK[2 * D + h * 128:2 * D + (h + 1) * 128, :]),
                (kx[:, 2560:2816], self.QK[r0:r0 + 128, NLAT:NTOK])], kr, reads=[self.r_QK, self.r_G], writes=[kr])
            vsl = slice(h * 128, (h + 1) * 128)
            self.DMAS("sp", [
                (vx[:, 0:2, :], GV[256:512, vsl].rearrange("(c p) d -> p c d", p=128)),
                (vx[:, 2:18, :], self.VD[0:NLAT, vsl].rearrange("(c p) d -> p c d", p=128)),
                (vx[:, 18:20, :], GV[512:768, vsl].rearrange("(c p) d -> p c d", p=128)),
                (vx[:, 20:22, :], self.VD[NLAT:NTOK, vsl].rearrange("(c p) d -> p c d", p=128))], vr,
                reads=[self.r_VD, self.r_G], writes=[vr])
            self.DMA("sp", qh, self.QK[h * 128:(h + 1) * 128, :], qr, reads=[self.r_QK], writes=[qr])
            self.DMA("sp", tb.rearrange("p a b c -> p (a b c)"), self.na_tab[j, h], tr, writes=[tr])
            nblk = 18 if need_ctx else 16
            for jb in range(nblk):
                if jb < 16:
                    ty = {0: 1, 1: 2, 14: 3, 15: 4}.get(jb, 0)
                    wc = {1: [0, 1, 2, 3, 4, 5], 2: [1, 2, 3, 4, 5], 3: [14, 15, 16, 17, 18], 4: [14, 15, 16, 17, 18, 19]}.get(ty, list(range(jb, jb + 5)))
                else:
                    wc = []
                nw = len(wc)
                tS = it % 2
                bS = (2 * tS, 2 * tS + 1)
                bO = 4 + (it % 2)
                it += 1
                psS = self.pst[tS]
                psO = self.banks[bO]
                qs = qh[:, jb * 128:(jb + 1) * 128]
                for s_, c in enumerate(wc):
                    self.MM(bS[s_ // 4], psS[:, s_ * 128:(s_ + 1) * 128], kx[:, c * 128:(c + 1) * 128], qs, True, True, [kr, qr])
                for s_, c in ((6, 20), (7, 21)):
                    self.MM(bS[1], psS[:, s_ * 128:(s_ + 1) * 128], kx[:, c * 128:(c + 1) * 128], qs, True, True, [kr, qr])
                pt, pr = pt_ring.next()
                if nw:
                    sb, sr = sb_ring.next()
                    self.STT(sb[:, 0:nw * 128], psS[:, 0:nw * 128], scale, tb[:, ty, 0:nw, :].rearrange("p a b -> p (a b)"),
                             ALU.mult, ALU.add, [self.bres[bS[0]], self.bres[bS[1]], tr], [sr])
                    self.ACT(pt[:, 0:nw * 128], sb[:, 0:nw * 128], AF.Exp, [sr], [pr])
                self.ACT(pt[:, 768:1024], psS[:, 768:1024], AF.Exp, [self.bres[bS[1]]], [pr], scale=scale)
                sl = [(s_, c) for s_, c in enumerate(wc)] + [(6, 20), (7, 21)]
                for i_, (s_, c) in enumerate(sl):
                    self.MM(bO, psO[:, 0:128], vx[:, c, :], pt[:, s_ * 128:(s_ + 1) * 128], i_ == 0, i_ == len(sl) - 1, [vr, pr])
                for i_, (s_, c) in enumerate(sl):
                    self.MM(bO, psO[:, 128:256], self.ones_bf, pt[:, s_ * 128:(s_ + 1) * 128], i_ == 0, i_ == len(sl) - 1, [pr, self.r_const])
                ri, rr = ri_ring.next()
                self.RECIP(ri, psO[:, 128:256], [self.bres[bO]], [rr])
                self.TT(oT[:, h, jb * 128:(jb + 1) * 128], psO[:, 0:128], ri, ALU.mult, [self.bres[bO], rr], [ores[h]])
        P.barrier()
        self.mixer_out(self.na_w_o[j], oT, ores, need_ctx)
        A.release(m0)

    def mixer_swa(self, ll, need_ctx):
        P, A = self.P, self.A
        m0 = A.mark()
        hT, hres = self.mixer_pre(ll)
        hr = lambda t0, n: hres[t0 // 128:(t0 + n) // 128]
        wq = self.swa_w_qkv
        rope = A.alloc([2, NLAT], F32)
        pit = A.alloc([128], F32)
        r_rp = Res("rope")
        self.DMA("sp", rope.rearrange("p a b -> p (a b)"), self.rope128, r_rp, writes=[r_rp])
        self.DMA("sp", pit, self.piT128, r_rp, writes=[r_rp])
        st_ring = Ring([A.alloc([512], BF16) for _ in range(4)], "st")
        qf_ring = Ring([A.alloc([512], F32) for _ in range(2)], "qf")
        t1_ring = Ring([A.alloc([512], F32) for _ in range(2)], "t1")
        t2_ring = Ring([A.alloc([512], F32) for _ in range(2)], "t2")
        cnt = [0]

        def ev_qk(ci, ti, ps, b):
            t0, n = TOK_TILES[ti]
            stg, sr = st_ring.next()
            if ti < 4:
                qf, qfr = qf_ring.next()
                t1, t1r = t1_ring.next()
                t2, t2r = t2_ring.next()
                bR = 6 + cnt[0] % 2
                cnt[0] += 1
                self.ACT(qf[:, 0:n], ps, AF.Identity, [self.bres[b]], [qfr])
                self.MM(bR, self.banks[bR][:, 0:n], pit, qf[:, 0:n], True, True, [qfr, r_rp])
                self.TT(t1[:, 0:n], qf[:, 0:n], rope[:, 0, t0:t0 + n], ALU.mult, [qfr, r_rp], [t1r], eng="pool")
                self.TT(t2[:, 0:n], self.banks[bR][:, 0:n], rope[:, 1, t0:t0 + n], ALU.mult, [self.bres[bR], r_rp], [t2r])
                self.TT(stg[:, 0:n], t1[:, 0:n], t2[:, 0:n], ALU.add, [t1r, t2r], [sr])
            else:
                self.ACT(stg[:, 0:n], ps, AF.Identity, [self.bres[b]], [sr])
            self.DMA("sp", self.QK[ci * 128:(ci + 1) * 128, t0:t0 + n], stg[:, 0:n], sr, reads=[sr], writes=[self.r_QK])
        self.linear_fm(wq, KC, hT, hr, [(c * 128, 128) for c in range(20)], TOK_TILES, ev_qk)

        def ev_v(t, cb, ps, b):
            stg, sr = st_ring.next()
            self.ACT(stg[:, 0:256], ps, AF.Identity, [self.bres[b]], [sr])
            self.DMA("sp", self.VD[t * 128:(t + 1) * 128, cb:cb + 256], stg[:, 0:256], sr, reads=[sr], writes=[self.r_VD])
        self.linear_tm(wq, KC, hT, hr, 2560, 512, NTOK // 128, ev_v)
        P.barrier()
        A.release(m0)
        m0 = A.mark()
        r_x = Res("swx")
        EXK, EXV, GK, GV = self.EXK2_t.ap(), self.EXV2_t.ap(), self.GK2_t.ap(), self.GV2_t.ap()
        for e_, tk in enumerate((0, NLAT - 128)):
            self.DMA("sp", EXK[e_ * 512:(e_ + 1) * 512, :], self.QK[D:D + 512, tk:tk + 128], r_x, reads=[self.r_QK], writes=[self.r_EX])
            self.DMA("sp", EXV[e_ * 128:(e_ + 1) * 128, :], self.VD[tk:tk + 128, 0:512], r_x, reads=[self.r_VD], writes=[self.r_EX])
        r_cc = Res("swcc")
        self.GATHER(self.EXK2_t, self.GK2_t, r_cc, [self.r_EX], [self.r_G])
        self.GATHER(self.EXV2_t, self.GV2_t, r_cc, [self.r_EX], [self.r_G])
        P.barrier()
        oT = A.alloc([KC, NTOK], BF16)
        ores = [Res("o%d" % h_) for h_ in range(4)]
        scale = 128.0 ** -0.5
        mask = A.alloc([5, 512], F32)
        sexp = A.alloc([16], F32)
        r_mk = Res("mk")
        self.DMA("sp", mask.rearrange("p a b -> p (a b)"), self.swa_mask, r_mk, writes=[r_mk])
        self.DMA("sp", sexp, self.swa_sink, r_mk, writes=[r_mk])
        self.ACT(sexp, sexp, AF.Exp, [r_mk], [r_mk])
        kx_ring = Ring([A.alloc([2560], BF16) for _ in range(2)], "kx")
        vx_ring = Ring([A.alloc([20, 128], BF16) for _ in range(2)], "vx")
        q_ring = Ring([A.alloc([18, 4, 128], BF16) for _ in range(2)], "qg")
        sb_ring = Ring([A.alloc([512], F32) for _ in range(2)], "sb")
        pt_ring = Ring([A.alloc([512], BF16) for _ in range(3)], "pt")
        dn_ring = Ring([A.alloc([512], F32) for _ in range(2)], "dn")
        it = 0
        si = 0
        for kvh in range(4):
            kx, kr = kx_ring.next()
            vx, vr = vx_ring.next()
            qg, qr = q_ring.next()
            r0 = D + kvh * 128
            self.DMAS("sp", [
                (kx[:, 0:128], GK[512 + kvh * 128:512 + (kvh + 1) * 128, :]),
                (kx[:, 128:2176], self.QK[r0:r0 + 128, 0:NLAT]),
                (kx[:, 2176:2304], GK[1024 + kvh * 128:1024 + (kvh + 1) * 128, :]),
                (kx[:, 2304:2560], self.QK[r0:r0 + 128, NLAT:NTOK])], kr, reads=[self.r_QK, self.r_G], writes=[kr])
            vsl = slice(kvh * 128, (kvh + 1) * 128)
            self.DMAS("sp", [
                (vx[:, 0:1, :], GV[128:256, vsl].rearrange("(c p) d -> p c d", p=128)),
                (vx[:, 1:17, :], self.VD[0:NLAT, vsl].rearrange("(c p) d -> p c d", p=128)),
                (vx[:, 17:18, :], GV[256:384, vsl].rearrange("(c p) d -> p c d", p=128)),
                (vx[:, 18:20, :], self.VD[NLAT:NTOK, vsl].rearrange("(c p) d -> p c d", p=128))], vr,
                reads=[self.r_VD, self.r_G], writes=[vr])
            self.DMAS("sp", [(qg[:, :, g, :], self.QK[(kvh * 4 + g) * 128:(kvh * 4 + g + 1) * 128, :].rearrange("p (n q) -> p n q", q=128))
                             for g in range(4)], qr, reads=[self.r_QK], writes=[qr])
            nblk = 18 if need_ctx else 16
            for n in range(nblk):
                if n < 16:
                    slots = [(n, (3 if n == 0 else 0)), (n + 1, None), (n + 2, (4 if n == 15 else 2)), (18, None), (19, None)]
                else:
                    slots = [(18, None), (19, None)]
                bO = 4 + 2 * (it % 2)
                bR = bO + 1
                it += 1
                psO, psR = self.banks[bO], self.banks[bR]
                qs = qg[:, n, :, :].rearrange("p a b -> p (a b)")
                pts = []
                for (c, mi) in slots:
                    bS = si % 4
                    si += 1
                    psS = self.banks[bS]
                    self.MM(bS, psS, kx[:, c * 128:(c + 1) * 128], qs, True, True, [kr, qr])
                    pt, pr = pt_ring.next()
                    if mi is None:
                        self.ACT(pt, psS, AF.Exp, [self.bres[bS]], [pr], scale=scale)
                    else:
                        sb, sr = sb_ring.next()
                        self.STT(sb, psS, scale, mask[:, mi, :], ALU.mult, ALU.add, [self.bres[bS], r_mk], [sr])
                        self.ACT(pt, sb, AF.Exp, [sr], [pr])
                    i_ = len(pts)
                    self.MM(bO, psO, vx[:, c, :], pt, i_ == 0, i_ == len(slots) - 1, [vr, pr])
                    self.MM(bR, psR, self.ones_bf, pt, i_ == 0, i_ == len(slots) - 1, [pr, self.r_const])
                    pts.append(pt)
                dn, dr = dn_ring.next()
                for g in range(4):
                    hh = kvh * 4 + g
                    self.TS(dn[:, g * 128:(g + 1) * 128], psR[:, g * 128:(g + 1) * 128], sexp[:, hh:hh + 1], None, ALU.add, None,
                            [self.bres[bR], r_mk], [dr])
                self.RECIP(dn, dn, [dr], [dr])
                self.TT(oT[:, kvh * 4:kvh * 4 + 4, n * 128:(n + 1) * 128], psO.rearrange("p (a b) -> p a b", a=4),
                        dn.rearrange("p (a b) -> p a b", a=4), ALU.mult, [self.bres[bO], dr], [ores[kvh]])
        P.barrier()
        self.mixer_out(self.swa_w_o, oT, ores, need_ctx)
        A.release(m0)

    def linear_tm2(self, w2d, kcn, in_buf, in_res, blocks, ntok128, evac, banks=(0, 1, 2, 3)):
        A = self.A
        m0 = A.mark()
        wring = Ring([A.alloc([kcn, 256], BF16) for _ in range(3)], "wt2")
        wv = w2d.rearrange("(kc p) n -> p kc n", p=128)
        it = 0
        for bi, (b0, bw) in enumerate(blocks):
            wt, wr = wring.next()
            self.DMA("pool", wt[:, :, 0:bw], wv[:, :, b0:b0 + bw], wr, writes=[wr])
            for t in range(ntok128):
                b = banks[it % len(banks)]
                it += 1
                ps = self.banks[b]
                for kc in range(kcn):
                    self.MM(b, ps[:, 0:bw], in_buf[:, kc, t * 128:(t + 1) * 128], wt[:, kc, 0:bw], kc == 0, kc == kcn - 1, [wr] + in_res)
                evac(t, bi, ps[:, 0:bw], b)
        self.P.barrier()
        A.release(m0)

    def mixer_mla(self, ll, need_ctx):
        P, A = self.P, self.A
        m0 = A.mark()
        EXL, GL = self.EXV_t.ap(), self.GV_t.ap()
        EXKv = self.EXK_t.ap()[0:512, :].rearrange("(f a) b -> f a b", a=8)
        GKa = self.GK_t.ap()
        CQ, CKC, KA, VA = self.CQ, self.CKC, self.KA, self.VA
        r_CQ, r_CKC, r_KA, r_VA = Res("CQ"), Res("CKC"), Res("KA"), Res("VA")
        rope = A.alloc([2, NLAT], F32)
        pit = A.alloc([128], F32)
        gq = A.alloc([8], F32)
        r_rp = Res("rope")
        self.DMA("sp", rope[0:64].rearrange("p a b -> p (a b)"), self.rope64, r_rp, writes=[r_rp])
        self.DMA("sp", pit, self.piT64, r_rp, writes=[r_rp])
        self.DMA("sp", gq, self.mla_gT, r_rp, writes=[r_rp])
        t1_ring = Ring([A.alloc([512], F32) for _ in range(2)], "t1")
        t2_ring = Ring([A.alloc([512], F32) for _ in range(2)], "t2")
        for t1_, t1r_ in zip(t1_ring.tiles, t1_ring.res):
            self.P.op("dve", (lambda t1_: lambda e: e.memset(t1_, 0.0))(t1_), writes=[t1r_])
        st_ring = Ring([A.alloc([512], BF16) for _ in range(4)], "st")
        m1 = A.mark()
        hT, hres = self.mixer_pre(ll)
        wd = A.alloc([KC, 1088], BF16)
        r_wd = Res("wd")
        wdv = self.mla_w_down.rearrange("(kc p) n -> p kc n", p=128)
        for q_ in range(4):
            self.DMA("pool", wd[:, :, q_ * 272:(q_ + 1) * 272], wdv[:, :, q_ * 272:(q_ + 1) * 272], r_wd, writes=[r_wd])
        cf_ring = Ring([A.alloc([4, 512], F32) for _ in range(2)], "cf")
        sq_ring = Ring([A.alloc([4, 512], BF16) for _ in range(2)], "sq")
        rs_ring = Ring([A.alloc([512], F32) for _ in range(2)], "rs")
        for ti, (t0, n) in enumerate(TOK_TILES):
            rr = hres[t0 // 128:(t0 + n) // 128]
            for grp in range(2):
                cf, cfr = cf_ring.next()
                sq, sqr = sq_ring.next()
                rs, rsr = rs_ring.next()
                for c in range(4):
                    col = grp * 512 + c * 128
                    for kc in range(KC):
                        self.MM(c, self.banks[c][:, 0:n], wd[:, kc, col:col + 128], hT[:, kc, t0:t0 + n], kc == 0, kc == KC - 1, [r_wd] + rr)
                    self.ACT(cf[:, c, 0:n], self.banks[c][:, 0:n], AF.Identity, [self.bres[c]], [cfr])
                self.ACT(sq[:, :, 0:n], cf[:, :, 0:n], AF.Square, [cfr], [sqr])
                bN = 4 + (ti * 2 + grp) % 2
                for c in range(4):
                    self.MM(bN, self.banks[bN][:, 0:n], self.inv512_bf, sq[:, c, 0:n], c == 0, c == 3, [sqr, self.r_const])
                self.ACT(rs[:, 0:n], self.banks[bN][:, 0:n], AF.Sqrt, [self.bres[bN]], [rsr], bias=EPS)
                self.RECIP(rs[:, 0:n], rs[:, 0:n], [rsr], [rsr])
                for c in range(4):
                    stg, sr = st_ring.next()
                    self.STT(stg[:, 0:n], cf[:, c, 0:n], gq[:, grp * 4 + c:grp * 4 + c + 1], rs[:, 0:n], ALU.mult, ALU.mult,
                             [cfr, rsr, r_rp], [sr])
                    if grp == 0:
                        self.DMA("sp", CQ[c * 128:(c + 1) * 128, t0:t0 + n], stg[:, 0:n], sr, reads=[sr], writes=[r_CQ])
                    elif ti < 4:
                        self.DMA("sp", EXL[c * 128:(c + 1) * 128, t0:t0 + n], stg[:, 0:n], sr, reads=[sr], writes=[self.r_EX])
                    else:
                        self.DMA("sp", CKC[c * 128:(c + 1) * 128, :], stg[:, 0:n], sr, reads=[sr], writes=[r_CKC])
            b = 6 + ti % 2
            for kc in range(KC):
                self.MM(b, self.banks[b][0:64, 0:n], wd[:, kc, 1024:1088], hT[:, kc, t0:t0 + n], kc == 0, kc == KC - 1, [r_wd] + rr)
            stg, sr = st_ring.next()
            self.rope64_apply(self.banks[b][0:64, 0:n], b, stg[0:64, 0:n], sr, ti, t0, n, rope, pit, r_rp, t1_ring, t2_ring)
            if ti < 4:
                self.DMA("sp", EXKv[:, t0 // 256:(t0 + n) // 256, :], stg[0:64, 0:n].rearrange("p (a b) -> p a b", b=256), sr,
                         reads=[sr], writes=[self.r_EX])
            else:
                self.DMA("sp", CKC[512:576, :], stg[0:64, 0:n], sr, reads=[sr], writes=[r_CKC])
        P.barrier()
        import os
        stop = os.environ.get("KMLA", "Z")
        if stop == "A":
            A.release(m0)
            return
        r_cc = Res("mlcc")
        self.GATHER(self.EXK_t, self.GK_t, r_cc, [self.r_EX], [self.r_G])
        self.GATHER(self.EXV_t, self.GV_t, r_cc, [self.r_EX], [self.r_G])
        P.barrier()
        A.release(m1)
        if stop == "B":
            A.release(m0)
            return
        cqn = A.alloc([4, NTOK], BF16)
        ckv_all = A.alloc([4, 4352], BF16)
        r_cqn, r_ckv = Res("cqn"), Res("ckv")
        self.DMAS("sp", [(cqn[:, c, :], CQ[c * 128:(c + 1) * 128, :]) for c in range(4)], r_cqn, reads=[r_CQ], writes=[r_cqn])
        pairs = []
        for c in range(4):
            for r_ in range(2):
                pairs.append((ckv_all[:, c, r_ * NLAT:(r_ + 1) * NLAT], GL[r_ * 512 + c * 128:r_ * 512 + (c + 1) * 128, :]))
            pairs.append((ckv_all[:, c, 2 * NLAT:2 * NLAT + NCTX], CKC[c * 128:(c + 1) * 128, :]))
        self.DMAS("sp", pairs, r_ckv, reads=[self.r_G, r_CKC], writes=[r_ckv])
        KT = [(k0, min(512, 4352 - k0)) for k0 in range(0, 4352, 512)]

        def ev_k(ci, ti, ps, b):
            k0, kn = KT[ti]
            stg, sr = st_ring.next()
            if ti % 2:
                self.ACT(stg[:, 0:kn], ps, AF.Identity, [self.bres[b]], [sr])
            else:
                self.CP(stg[:, 0:kn], ps, [self.bres[b]], [sr])
            self.DMA("sp", KA[ci * 128:(ci + 1) * 128, k0:k0 + kn], stg[:, 0:kn], sr, reads=[sr], writes=[r_KA])
        self.linear_fm(self.mla_w_kv_up, 4, ckv_all, lambda t0, n: [r_ckv], [(h * 256, 128) for h in range(16)], KT, ev_k)

        def ev_v(t, bi, ps, b):
            stg, sr = st_ring.next()
            if t % 2:
                self.ACT(stg[:, 0:128], ps, AF.Identity, [self.bres[b]], [sr])
            else:
                self.CP(stg[:, 0:128], ps, [self.bres[b]], [sr])
            self.DMA("sp", VA[t * 128:(t + 1) * 128, bi * 128:(bi + 1) * 128], stg[:, 0:128], sr, reads=[sr], writes=[r_VA])
        self.linear_tm2(self.mla_w_kv_up, 4, ckv_all, [r_ckv], [(h * 256 + 128, 128) for h in range(16)], 34, ev_v)

        def ev_qn(ci, ti, ps, b):
            t0, n = TOK_TILES[ti]
            stg, sr = st_ring.next()
            self.ACT(stg[:, 0:n], ps, AF.Identity, [self.bres[b]], [sr])
            self.DMA("sp", self.QK[ci * 128:(ci + 1) * 128, t0:t0 + n], stg[:, 0:n], sr, reads=[sr], writes=[self.r_QK])
        self.linear_fm(self.mla_w_q_up, 4, cqn, lambda t0, n: [r_cqn], [(h * 192, 128) for h in range(16)], TOK_TILES, ev_qn)

        def ev_qp(ci, ti, ps, b):
            t0, n = TOK_TILES[ti]
            stg, sr = st_ring.next()
            self.rope64_apply(ps, b, stg[0:64, 0:n], sr, ti, t0, n, rope, pit, r_rp, t1_ring, t2_ring)
            self.DMA("sp", self.QK[D + ci * 64:D + (ci + 1) * 64, t0:t0 + n], stg[0:64, 0:n], sr, reads=[sr], writes=[self.r_QK])
        self.linear_fm(self.mla_w_q_up, 4, cqn, lambda t0, n: [r_cqn], [(h * 192 + 128, 64) for h in range(16)], TOK_TILES, ev_qp)
        P.barrier()
        A.release(m0)
        if stop == "C":
            return
        m0 = A.mark()
        oT = A.alloc([KC, NTOK], BF16)
        ores = [Res("o%d" % h_) for h_ in range(16)]
        scale = 192.0 ** -0.5
        kp_all = A.alloc([4352], BF16)
        r_kp = Res("kp")
        self.DMAS("sp", [(kp_all[0:64, 0:NLAT], GKa[0:512, :].rearrange("(f a) b -> f (a b)", a=8)),
                         (kp_all[0:64, NLAT:2 * NLAT], GKa[2 * D:2 * D + 512, :].rearrange("(f a) b -> f (a b)", a=8)),
                         (kp_all[0:64, 2 * NLAT:2 * NLAT + NCTX], CKC[512:576, :])], r_kp, reads=[self.r_G, r_CKC], writes=[r_kp])
        k_ring = Ring([A.alloc([4352], BF16) for _ in range(2)], "Kh")
        v_ring = Ring([A.alloc([34, 128], BF16) for _ in range(2)], "Vh")
        qn_ring = Ring([A.alloc([NTOK], BF16) for _ in range(2)], "Qn")
        qp_ring = Ring([A.alloc([NTOK], BF16) for _ in range(2)], "Qp")
        pt_ring = Ring([A.alloc([512], BF16) for _ in range(3)], "pt")
        ri_ring = Ring([A.alloc([512], F32) for _ in range(2)], "ri")
        it = 0
        io = 0
        for h in range(16):
            Kh, r_K = k_ring.next()
            Vh, r_V = v_ring.next()
            Qn, r_Qn = qn_ring.next()
            Qp, r_Qp = qp_ring.next()
            self.DMA("sp", Kh, KA[h * 128:(h + 1) * 128, :], r_K, reads=[r_KA], writes=[r_K])
            self.DMAS("sp", [(Vh[:, c0:c1, :], VA[c0 * 128:c1 * 128, h * 128:(h + 1) * 128].rearrange("(c p) d -> p c d", p=128))
                             for (c0, c1) in ((0, 12), (12, 24), (24, 34))], r_V, reads=[r_VA], writes=[r_V])
            self.DMA("sp", Qn, self.QK[h * 128:(h + 1) * 128, :], r_Qn, reads=[self.r_QK], writes=[r_Qn])
            self.DMA("sp", Qp[0:64], self.QK[D + h * 64:D + (h + 1) * 64, :], r_Qp, reads=[self.r_QK], writes=[r_Qp])
            qgs = [(q0, 512, list(range(34))) for q0 in range(0, NLAT, 512)]
            if need_ctx:
                qgs.append((NLAT, NCTX, [32, 33]))
            for (q0, qn, kcs) in qgs:
                bO = 4 + 2 * (io % 2)
                bR = bO + 1
                io += 1
                psO, psR = self.banks[bO], self.banks[bR]
                for i_, kc in enumerate(kcs):
                    b = it % 4
                    it += 1
                    psS = self.banks[b]
                    self.MM(b, psS[:, 0:qn], Kh[:, kc * 128:(kc + 1) * 128], Qn[:, q0:q0 + qn], True, False, [r_K, r_Qn])
                    self.MM(b, psS[:, 0:qn], kp_all[0:64, kc * 128:(kc + 1) * 128], Qp[0:64, q0:q0 + qn], False, True, [r_kp, r_Qp])
                    pt, pr = pt_ring.next()
                    self.ACT(pt[:, 0:qn], psS[:, 0:qn], AF.Exp, [self.bres[b]], [pr], scale=scale)
                    self.MM(bO, psO[:, 0:qn], Vh[:, kc, :], pt[:, 0:qn], i_ == 0, i_ == len(kcs) - 1, [r_V, pr])
                    self.MM(bR, psR[:, 0:qn], self.ones_bf, pt[:, 0:qn], i_ == 0, i_ == len(kcs) - 1, [pr, self.r_const])
                ri, rr = ri_ring.next()
                self.RECIP(ri[:, 0:qn], psR[:, 0:qn], [self.bres[bR]], [rr])
                self.TT(oT[:, h, q0:q0 + qn], psO[:, 0:qn], ri[:, 0:qn], ALU.mult, [self.bres[bO], rr], [ores[h]])
        P.barrier()
        self.mixer_out(self.mla_w_o, oT, ores, need_ctx)
        A.release(m0)

    def rope64_apply(self, ps, b, dst, dres, ti, t0, n, rope, pit, r_rp, t1_ring, t2_ring):
        if ti >= 4:
            self.ACT(dst, ps, AF.Identity, [self.bres[b]], [dres])
            return
        t1, t1r = t1_ring.next()
        t2, t2r = t2_ring.next()
        bR = 6 + ti % 2
        self.ACT(t1[0:64, 0:n], ps, AF.Identity, [self.bres[b]], [t1r])
        self.MM(bR, self.banks[bR][:, 0:n], pit, t1[:, 0:n], True, True, [t1r, r_rp])
        self.TT(t2[0:64, 0:n], self.banks[bR][0:64, 0:n], rope[0:64, 1, t0:t0 + n], ALU.mult, [self.bres[bR], r_rp], [t2r])
        self.TT(t1[0:64, 0:n], t1[0:64, 0:n], rope[0:64, 0, t0:t0 + n], ALU.mult, [t1r, r_rp], [t1r])
        self.TT(dst, t1[0:64, 0:n], t2[0:64, 0:n], ALU.add, [t1r, t2r], [dres])

_CACHE = {}
NCORES = 8
PLAN = [[0], [1], [2], [3]]


def _fm(a):
    return np.ascontiguousarray(np.asarray(a, np.float32).reshape(-1, 128).T)


def _na_tables(rpb, half):
    types = [(5, [5, 6, 7, 8, 9]), (0, [0, 1, 2, 3, 4, 5]), (1, [1, 2, 3, 4, 5]), (14, [14, 15, 16, 17, 18]), (15, [14, 15, 16, 17, 18, 19])]
    tab = np.full((16, 128, 5, 6, 128), NEG, np.float32)
    p = np.arange(128)[:, None]
    q = np.arange(128)[None, :]
    for ti, (j, wc) in enumerate(types):
        qt = half * NLAT + j * 128 + q
        r, qc = qt // 64, qt % 64
        rs = np.clip(r - 4, 0, 56)
        cs = np.clip(qc - 8, 0, 48)
        for s_, c in enumerate(wc):
            kt = half * NLAT + c * 128 - 256 + p
            valid = (kt >= 0) & (kt < 2 * NLAT)
            kr, kcol = kt // 64, kt % 64
            ok = valid & (kr >= rs) & (kr < rs + 8) & (kcol >= cs) & (kcol < cs + 16)
            ri = np.clip(kr - r + 7, 0, 14)
            ci = np.clip(kcol - qc + 15, 0, 30)
            vals = rpb[:, ri, ci]
            tab[:, :, ti, s_, :] = np.where(ok[None], vals, np.float32(NEG))
    return tab.reshape(16, 128, 5 * 6 * 128)


def _rope_tab(half, rot_dim, nparts):
    t = np.arange(NLAT) + half * NLAT
    row = (t // 64).astype(np.float32)
    col = (t % 64).astype(np.float32)
    nf = rot_dim // 4
    inv = (np.float32(10000.0) ** (-np.arange(nf, dtype=np.float32) / np.float32(nf))).astype(np.float32)
    ang = np.concatenate([row[:, None] * inv, col[:, None] * inv], axis=-1).astype(np.float32)
    hd = rot_dim // 2
    idx = np.arange(nparts) % hd
    C = np.cos(ang)[:, idx].T.astype(np.float32)
    S = np.sin(ang)[:, idx].T.astype(np.float32)
    return np.ascontiguousarray(np.concatenate([C, S], axis=1))


def _pit(n):
    h = n // 2
    m = np.zeros((n, n), np.float32)
    for mm in range(n):
        if mm < h:
            m[mm + h, mm] = -1.0
        else:
            m[mm - h, mm] = 1.0
    return m


def _swa_mask(half):
    p = np.arange(128)[:, None]
    q = np.arange(128)[None, :]
    z = np.zeros((128, 128), np.float32)
    neg = np.full((128, 128), NEG, np.float32)
    m0 = np.where(p >= q, z, neg)
    m2 = np.where(p <= q, z, neg)
    ms = [m0, z, m2, (m0 if half == 1 else neg), (m2 if half == 0 else neg)]
    out = np.stack([np.broadcast_to(m[:, None, :], (128, 4, 128)) for m in ms], axis=1)
    return np.ascontiguousarray(out.reshape(128, 5 * 4 * 128).astype(np.float32))


def _run_group(inp, layers, stream, last):
    key = (tuple(layers), last, NCORES)
    if key not in _CACHE:
        _CACHE[key] = Builder(list(layers), True, last).build()
    nc = _CACHE[key]
    c = np.asarray(inp["c"], np.float32)
    c_ctx = np.asarray(inp["c_ctx"], np.float32)
    kinds = [li % 3 for li in layers]
    na_js = [li // 3 for li in layers if li % 3 == 0]
    in_maps = []
    for core in range(NCORES):
        b, h = core // 2, core % 2
        m = {
            "xT_in": stream[core],
            "csT": np.ascontiguousarray(np.stack([_fm(c[b]), _fm(c_ctx)], axis=-1).reshape(128, KC * 2)),
            "w_mod": inp["w_mod"][layers],
            "b_modT": np.stack([_fm(inp["b_mod"][l]) for l in layers]),
            "norm_gT": np.stack([np.concatenate([_fm(inp["norm_g"][l, s]) for s in range(3)], axis=1) for l in layers]),
            "ffn_w_in": inp["ffn_w_in"][layers],
            "ffn_w_out": inp["ffn_w_out"][layers],
        }
        if na_js:
            m["na_w_qkv"] = inp["na_w_qkv"][na_js]
            m["na_w_o"] = inp["na_w_o"][na_js]
            m["na_tab"] = np.stack([_na_tables(np.asarray(inp["na_rpb"][j], np.float32), h) for j in na_js])
        if 1 in kinds:
            m["mla_w_down"] = inp["mla_w_down"][0]
            m["mla_w_q_up"] = inp["mla_w_q_up"][0]
            m["mla_w_kv_up"] = inp["mla_w_kv_up"][0]
            m["mla_w_o"] = inp["mla_w_o"][0]
            m["mla_gT"] = np.ascontiguousarray(np.concatenate([_fm(inp["mla_q_norm_g"][0]), _fm(inp["mla_kv_norm_g"][0])], axis=1))
            m["rope64"] = _rope_tab(h, 64, 64)
            pp = np.zeros((128, 128), np.float32)
            pp[0:64, 0:64] = _pit(64)
            m["piT64"] = pp
        if 2 in kinds:
            m["swa_w_qkv"] = inp["swa_w_qkv"][0]
            m["swa_w_o"] = inp["swa_w_o"][0]
            m["swa_mask"] = _swa_mask(h)
            m["swa_sink"] = np.ascontiguousarray(np.broadcast_to(np.asarray(inp["swa_sink"][0], np.float32)[None, :], (128, 16)))
            m["rope128"] = _rope_tab(h, 128, 128)
            m["piT128"] = _pit(128)
        if last:
            m["final_gT"] = _fm(inp["final_norm_g"])
        in_maps.append(m)
    res = run_bass_kernel_spmd(nc, in_maps, core_ids=list(range(NCORES)))
    global LAST_RES
    LAST_RES = res
    return [r["yT" if last else "xT_out"] for r in res.results]


def kernel(**inp):
    x = np.asarray(inp["x"], np.float32)
    ctx = np.asarray(inp["ctx"], np.float32)
    stream = []
    for core in range(NCORES):
        b, h = core // 2, core % 2
        stream.append(np.ascontiguousarray(np.concatenate([x[b, h * NLAT:(h + 1) * NLAT].T, ctx[b].T], axis=1)))
    for gi, layers in enumerate(PLAN):
        last = layers[-1] == DEPTH - 1
        stream = _run_group(inp, layers, stream, last)
    out = np.zeros((NCORES // 2, 2 * NLAT, D), np.float32)
    for core in range(NCORES):
        b, h = core // 2, core % 2
        out[b, h * NLAT:(h + 1) * NLAT, :] = stream[core].T
    return out
```
